# Optimizing a Trainium2 kernel written in Bass

```python
import jax
import jax.numpy as jnp
from jax import lax
import numpy as np


D_MODEL = 1024
BATCH = 2
SEQ = 8192
DEPTH = 1

HEAD_DIM = 64
NA_HEADS = 8
NA_WIDTH = NA_HEADS * HEAD_DIM
NA_ROWS = 8
NA_COLS = 16
GRID_W = 64
DIL_PAIRS = ((128, 1), (512, 4), (2048, 16))
NB_GROUPS = len(DIL_PAIRS)
NB_HEADS_PER_GROUP = 4
NB_HEADS = NB_GROUPS * NB_HEADS_PER_GROUP
NB_WIDTH = NB_HEADS * HEAD_DIM
ALIBI_MAX_EXP = 8.0
LOCAL_QBLOCK = 64
D_FF = 2816
N_IN = 3 * NA_WIDTH + 3 * NB_WIDTH + 2 * D_MODEL
NORM_EPS = 1e-6
ATTN_SCALE = HEAD_DIM ** -0.5

kernel_name = 'hybrid_natten_dilated_macaron_block'


def _rmsnorm(x, g):
    xf = x.astype(jnp.float32)
    y = xf * lax.rsqrt(jnp.mean(xf * xf, axis=-1, keepdims=True) + NORM_EPS)
    return (y * g.astype(jnp.float32)).astype(x.dtype)


def _swiglu(x, w_gate, w_up, w_down):
    return (jax.nn.silu(x @ w_gate) * (x @ w_up)) @ w_down


def _neighbourhood_attention(q, k, v, rpb):
    B, T, H, Dh = q.shape
    rows = T // GRID_W
    kh = min(NA_ROWS, rows)
    r = jnp.arange(rows)
    row0 = jnp.clip(r - kh // 2, 0, rows - kh)
    key_rows = row0[:, None] + jnp.arange(kh)[None, :]
    c = jnp.arange(GRID_W)
    col0 = jnp.clip(c - NA_COLS // 2, 0, GRID_W - NA_COLS)
    col_ok = (c[None, :] >= col0[:, None]) & (c[None, :] < col0[:, None] + NA_COLS)
    col_rel = jnp.clip(c[None, :] - c[:, None], -(NA_COLS - 1), NA_COLS - 1)
    row_rel = key_rows - r[:, None]
    bias = rpb[:, (row_rel + NA_ROWS - 1)[:, None, :, None],
               (col_rel + NA_COLS - 1)[None, :, None, :]]
    qg = q.reshape(B, rows, GRID_W, H, Dh)
    kg = k.reshape(B, rows, GRID_W, H, Dh)[:, key_rows]
    vg = v.reshape(B, rows, GRID_W, H, Dh)[:, key_rows]
    s = jnp.einsum('brqhd,brkwhd->bhrqkw', qg, kg).astype(jnp.float32) * ATTN_SCALE
    s = s + bias[None].astype(jnp.float32)
    s = jnp.where(col_ok[:, None, :], s, -jnp.inf)
    p = jax.nn.softmax(s.reshape(B, H, rows, GRID_W, kh * GRID_W), axis=-1)
    p = p.reshape(s.shape).astype(v.dtype)
    o = jnp.einsum('bhrqkw,brkwhd->brqhd', p, vg)
    return o.reshape(B, T, H * Dh)


def _banded_attention(q, k, v, half, dilation, slopes):
    N, H, L, Dh = q.shape
    nb = -(-L // LOCAL_QBLOCK)
    lp = nb * LOCAL_QBLOCK
    pad = lp - L
    qp = jnp.pad(q, ((0, 0), (0, 0), (0, pad), (0, 0))).reshape(N, H, nb, LOCAL_QBLOCK, Dh)
    kp = jnp.pad(k, ((0, 0), (0, 0), (half, half + pad), (0, 0)))
    vp = jnp.pad(v, ((0, 0), (0, 0), (half, half + pad), (0, 0)))
    kspan = LOCAL_QBLOCK + 2 * half
    kidx = jnp.arange(nb)[:, None] * LOCAL_QBLOCK + jnp.arange(kspan)[None, :]
    kblk = kp[:, :, kidx]
    vblk = vp[:, :, kidx]
    s = jnp.einsum('nhbqd,nhbkd->nhbqk', qp, kblk).astype(jnp.float32) * ATTN_SCALE
    qpos = jnp.arange(nb)[:, None] * LOCAL_QBLOCK + jnp.arange(LOCAL_QBLOCK)[None, :]
    kpos = kidx - half
    rel = kpos[:, None, :] - qpos[:, :, None]
    valid = ((jnp.abs(rel) <= half) & (kpos[:, None, :] >= 0) & (kpos[:, None, :] < L)) | (qpos[:, :, None] >= L)
    dist = (dilation * jnp.abs(rel)).astype(jnp.float32)
    s = s - slopes[:, None, None, None] * dist[None]
    s = jnp.where(valid, s, -jnp.inf)
    lse = jax.nn.logsumexp(s, axis=-1)
    p = jnp.exp(s - lse[..., None]).astype(v.dtype)
    o = jnp.einsum('nhbqk,nhbkd->nhbqd', p, vblk).reshape(N, H, lp, Dh)[:, :, :L]
    return o, lse.reshape(N, H, lp)[:, :, :L]


def _dilated_attention(q, k, v, dilation, half, slopes):
    B, T, H, Dh = q.shape
    L = T // dilation

    def to_res(a):
        return a.reshape(B, L, dilation, H, Dh).transpose(0, 2, 3, 1, 4).reshape(B * dilation, H, L, Dh)

    o, lse = _banded_attention(to_res(q), to_res(k), to_res(v), half, dilation, slopes)
    o = o.reshape(B, dilation, H, L, Dh).transpose(0, 3, 1, 2, 4).reshape(B, T, H, Dh)
    lse = lse.reshape(B, dilation, H, L).transpose(0, 3, 1, 2).reshape(B, T, H)
    return o, lse


def _dilated_mixture(q, k, v, slopes):
    B, T, _ = q.shape
    shp = (B, T, NB_GROUPS, NB_HEADS_PER_GROUP, HEAD_DIM)
    q, k, v = q.reshape(shp), k.reshape(shp), v.reshape(shp)
    outs, lses = [], []
    for g, (window, dilation) in enumerate(DIL_PAIRS):
        half = window // (2 * dilation)
        sl = slopes[g * NB_HEADS_PER_GROUP:(g + 1) * NB_HEADS_PER_GROUP]
        o, lse = _dilated_attention(q[:, :, g], k[:, :, g], v[:, :, g], dilation, half, sl)
        outs.append(o)
        lses.append(lse)
    alpha = jax.nn.softmax(jnp.stack(lses, axis=0), axis=0)
    o = jnp.stack(outs, axis=0) * alpha[..., None].astype(outs[0].dtype)
    return o.transpose(1, 2, 0, 3, 4).reshape(B, T, NB_WIDTH)


def setup_inputs(seed: int = 0) -> dict:
    key = jax.random.key(seed)
    ks = jax.random.split(key, 20)

    def nrm(k, shape, scale):
        return jax.random.normal(k, shape, jnp.float32) * scale

    def gain(k):
        return 1.0 + 0.05 * jax.random.normal(k, (DEPTH, D_MODEL), jnp.float32)

    return {
        'x': nrm(ks[0], (BATCH, SEQ, D_MODEL), 1.0),
        'ffn1_pre_g': gain(ks[1]),
        'ffn1_w_gate': nrm(ks[2], (DEPTH, D_MODEL, D_FF), D_MODEL ** -0.5),
        'ffn1_w_up': nrm(ks[3], (DEPTH, D_MODEL, D_FF), D_MODEL ** -0.5),
        'ffn1_w_down': nrm(ks[4], (DEPTH, D_FF, D_MODEL), D_FF ** -0.5),
        'ffn1_post_g': gain(ks[5]),
        'mix_pre_g': gain(ks[6]),
        'w_in': nrm(ks[7], (DEPTH, D_MODEL, N_IN), D_MODEL ** -0.5),
        'na_rpb': nrm(ks[8], (DEPTH, NA_HEADS, 2 * NA_ROWS - 1, 2 * NA_COLS - 1), 0.1),
        'w_branch_a': nrm(ks[9], (DEPTH, NA_WIDTH, D_MODEL), NA_WIDTH ** -0.5),
        'w_branch_b': nrm(ks[10], (DEPTH, NB_WIDTH, D_MODEL), NB_WIDTH ** -0.5),
        'w_out': nrm(ks[11], (DEPTH, D_MODEL, D_MODEL), D_MODEL ** -0.5),
        'mix_post_g': gain(ks[12]),
        'ffn2_pre_g': gain(ks[13]),
        'ffn2_w_gate': nrm(ks[14], (DEPTH, D_MODEL, D_FF), D_MODEL ** -0.5),
        'ffn2_w_up': nrm(ks[15], (DEPTH, D_MODEL, D_FF), D_MODEL ** -0.5),
        'ffn2_w_down': nrm(ks[16], (DEPTH, D_FF, D_MODEL), D_FF ** -0.5),
        'ffn2_post_g': gain(ks[17]),
    }


def reference(x, ffn1_pre_g, ffn1_w_gate, ffn1_w_up, ffn1_w_down, ffn1_post_g,
              mix_pre_g, w_in, na_rpb, w_branch_a, w_branch_b, w_out, mix_post_g,
              ffn2_pre_g, ffn2_w_gate, ffn2_w_up, ffn2_w_down, ffn2_post_g):
    B, T, _ = x.shape
    slopes = jnp.exp2(-ALIBI_MAX_EXP * (jnp.arange(NB_HEADS, dtype=jnp.float32) + 1.0) / NB_HEADS)
    widths = [NA_WIDTH] * 3 + [NB_WIDTH] * 3 + [D_MODEL]
    splits = [int(s) for s in np.cumsum(widths)]
    for l in range(DEPTH):
        f = _swiglu(_rmsnorm(x, ffn1_pre_g[l]), ffn1_w_gate[l], ffn1_w_up[l], ffn1_w_down[l])
        x = x + 0.5 * _rmsnorm(f, ffn1_post_g[l])
        h = _rmsnorm(x, mix_pre_g[l])
        proj = h @ w_in[l]
        qa, ka, va, qb, kb, vb, gate_a, gate_b = jnp.split(proj, splits, axis=-1)
        ha = (B, T, NA_HEADS, HEAD_DIM)
        ya = _neighbourhood_attention(qa.reshape(ha), ka.reshape(ha), va.reshape(ha), na_rpb[l]) @ w_branch_a[l]
        yb = _dilated_mixture(qb, kb, vb, slopes) @ w_branch_b[l]
        merged = jax.nn.sigmoid(gate_a) * ya + jax.nn.sigmoid(gate_b) * yb
        x = x + _rmsnorm(merged @ w_out[l], mix_post_g[l])
        f = _swiglu(_rmsnorm(x, ffn2_pre_g[l]), ffn2_w_gate[l], ffn2_w_up[l], ffn2_w_down[l])
        x = x + 0.5 * _rmsnorm(f, ffn2_post_g[l])
    return x
```

```python
import numpy as np
import concourse.bass as bass
import concourse.mybir as mybir
from concourse.bass_utils import run_bass_kernel_spmd

F32 = mybir.dt.float32
BF16 = mybir.dt.bfloat16
AF = mybir.ActivationFunctionType
ALU = mybir.AluOpType

NCORES = 8
D = 1024
DFF = 2816
T = 8192
OWN = 2048
HALO = 1024
EXT = OWN + 2 * HALO
TB = 512
KC = D // 128
MC = DFF // 128
NEG = -30000.0
EPS = 1e-6
DIL = (1, 4, 16)
SLOT = 6656
NSLOT = 3

NA_OFF = {0: 0, 1: 6, 14: 16, 15: 21}


def ss(start, step, n=128):
    return slice(start, start + (n - 1) * step + 1, step)


def na_chunks(j):
    if j == 0:
        return list(range(0, 6)), 0
    if j == 1:
        return list(range(1, 6)), 6
    if j == 14:
        return list(range(14, 19)), 16
    if j == 15:
        return list(range(14, 20)), 21
    return list(range(j, j + 5)), 11


def _kmajor(w):
    K, n = w.shape
    return np.ascontiguousarray(w.reshape(K // 128, 128, n).transpose(1, 0, 2)).reshape(128, -1)


def weight_tiles(inp):
    tiles = []

    def add(name, arr):
        tiles.append((name, arr))

    for f in (1, 2):
        wg = inp[f"ffn{f}_w_gate"][0]
        wu = inp[f"ffn{f}_w_up"][0]
        wd = inp[f"ffn{f}_w_down"][0]
        for mg in range(MC // 2):
            cs = slice(mg * 256, (mg + 1) * 256)
            add(f"f{f}gu{mg}", np.concatenate([_kmajor(wg[:, cs]), _kmajor(wu[:, cs])], axis=1))
        for dg in range(4):
            cs = slice(dg * 256, (dg + 1) * 256)
            add(f"f{f}d{dg}", _kmajor(wd[:, cs]))
    win = inp["w_in"][0]
    for hp in range(4):
        cols = np.concatenate([np.arange(hp * 128, hp * 128 + 128) + base for base in (0, 512, 1024)])
        add(f"na{hp}", _kmajor(win[:, cols]))
    for g in range(3):
        cols = np.concatenate([np.arange(g * 256, g * 256 + 256) + base for base in (1536, 2304, 3072)])
        add(f"dl{g}", _kmajor(win[:, cols]))
    wa = inp["w_branch_a"][0]
    wb = inp["w_branch_b"][0]
    for dg in range(4):
        cs = slice(dg * 256, (dg + 1) * 256)
        add(f"mixc{dg}", np.concatenate([
            _kmajor(wa[:, cs]), _kmajor(wb[:, cs]),
            _kmajor(win[:, 3840 + dg * 256: 3840 + (dg + 1) * 256]),
            _kmajor(win[:, 4864 + dg * 256: 4864 + (dg + 1) * 256])], axis=1))
    wo = inp["w_out"][0]
    for hf in range(2):
        add(f"wout{hf}", _kmajor(wo[:, hf * 512:(hf + 1) * 512]))
    offs = {}
    off = 0
    for name, arr in tiles:
        offs[name] = (off, arr.shape[1])
        off += arr.shape[1]
    pk = np.concatenate([a for _, a in tiles], axis=1).astype(np.float32)
    return offs, pk


def tile_table():
    offs = {}
    off = 0

    def add(name, n):
        nonlocal off
        offs[name] = (off, n)
        off += n

    for f in (1, 2):
        for mg in range(MC // 2):
            add(f"f{f}gu{mg}", 2 * KC * 256)
        for dg in range(4):
            add(f"f{f}d{dg}", MC * 256)
    for hp in range(4):
        add(f"na{hp}", KC * 384)
    for g in range(3):
        add(f"dl{g}", KC * 768)
    for dg in range(4):
        add(f"mixc{dg}", 26 * 256)
    for hf in range(2):
        add(f"wout{hf}", KC * 512)
    return offs, off


def na_bias_table(rpb, qd):
    out = np.full((128, 8, 27, 128), NEG, np.float32)
    i = np.arange(128)
    ki, kc = i // 64, i % 64
    qi, qc = i // 64, i % 64
    for j in (0, 1, 2, 14, 15):
        chunks, off = na_chunks(j)
        Rq = 32 * qd + 2 * j + qi
        row0 = np.clip(Rq - 4, 0, 120)
        col0 = np.clip(qc - 8, 0, 48)
        for ci, cg in enumerate(chunks):
            Rk = 32 * qd + 2 * cg - 4 + ki
            vr = (Rk[:, None] >= row0[None, :]) & (Rk[:, None] < row0[None, :] + 8) \
                & (Rk[:, None] >= 0) & (Rk[:, None] < 128)
            vc = (kc[:, None] >= col0[None, :]) & (kc[:, None] < col0[None, :] + 16)
            valid = vr & vc
            rr = np.clip(Rk[:, None] - Rq[None, :] + 7, 0, 14)
            cr = np.clip(kc[:, None] - qc[None, :], -15, 15) + 15
            vals = rpb[:, rr, cr]
            blk = np.where(valid[None], vals, np.float32(NEG)).astype(np.float32)
            out[:, :, off + ci, :] = blk.transpose(1, 0, 2)
    return out.reshape(128, -1)


def dil_bias_table(qd):
    out = np.empty((128, 12, 4, 128), np.float32)
    slopes = np.exp2(-8.0 * (np.arange(12, dtype=np.float32) + 1.0) / 12.0).astype(np.float32)
    i = np.arange(128)[:, None]
    q = np.arange(128)[None, :]
    for h in range(12):
        d = DIL[h // 4]
        for v, du in ((0, i - 64 - q), (2, i + 64 - q)):
            dist = (d * np.abs(du)).astype(np.float32)
            b = np.where(np.abs(du) <= 64, -(slopes[h] * dist), np.float32(NEG)).astype(np.float32)
            out[:, h, v, :] = b
            b2 = b.copy()
            if v == 0 and qd == 0:
                b2[:64, :] = NEG
            if v == 2 and qd == 3:
                b2[64:, :] = NEG
            out[:, h, v + 1, :] = b2
    return out.reshape(128, -1)


class Eng:
    def __init__(self, nc, e, name, own_wait):
        self.e = e
        self.name = name
        self.sem = nc.alloc_semaphore("es_" + name)
        self.cnt = 0
        self.seen = {}
        self.own_wait = own_wait

    def wait(self, ev):
        if ev is None:
            return
        sem, val, key = ev
        if key == self.name and not self.own_wait:
            return
        if self.seen.get(key, 0) >= val:
            return
        self.e.wait_ge(sem, val)
        self.seen[key] = val

    def signal(self, inst):
        self.cnt += 1
        inst.then_inc(self.sem, 1)
        return (self.sem, self.cnt, self.name)


class Buf:
    __slots__ = ("name", "w", "r", "dsem", "dcnt")

    def __init__(self, name):
        self.name = name
        self.w = None
        self.r = {}
        self.dsem = None
        self.dcnt = 0


class Prog:
    def __init__(self, debug=False):
        nc = bass.Bass("TRN2", target_bir_lowering=False)
        self.nc = nc
        self.debug = debug
        self.PE = Eng(nc, nc.tensor, "pe", False)
        self.ACT = Eng(nc, nc.scalar, "act", True)
        self.DVE = Eng(nc, nc.vector, "dve", True)
        self.SP = Eng(nc, nc.sync, "sp", False)
        self.POOL = Eng(nc, nc.gpsimd, "pool", False)
        self.engs = [self.PE, self.ACT, self.DVE, self.SP, self.POOL]
        self.dbufs = []
        self.nbuf = 0
        self.evq = 0

    def buf(self, name=None):
        self.nbuf += 1
        return Buf(name or f"b{self.nbuf}")

    def bufs(self, n, name=None):
        return [self.buf(f"{name}{i}" if name else None) for i in range(n)]

    def _waits(self, E, reads, writes):
        for b in reads:
            E.wait(b.w)
        for b in writes:
            E.wait(b.w)
            for ev in b.r.values():
                E.wait(ev)

    def _done(self, ev, reads, writes):
        for b in reads:
            b.r[ev[2]] = ev
        for b in writes:
            b.w = ev
            b.r = {}

    def op(self, E, emit, reads=(), writes=()):
        self._waits(E, reads, writes)
        inst = emit()
        ev = E.signal(inst)
        self._done(ev, reads, writes)
        return inst

    def mm(self, out, pairs, reads=(), writes=()):
        self._waits(self.PE, reads, writes)
        n = len(pairs)
        inst = None
        for i, (l, r) in enumerate(pairs):
            inst = self.nc.tensor.matmul(out, l, r, start=(i == 0), stop=(i == n - 1))
        ev = self.PE.signal(inst)
        self._done(ev, reads, writes)

    def mm_multi(self, items, reads=(), writes=()):
        self._waits(self.PE, reads, writes)
        inst = None
        for (out, l, r) in items:
            inst = self.nc.tensor.matmul(out, l, r, start=True, stop=True)
        ev = self.PE.signal(inst)
        self._done(ev, reads, writes)

    def dma(self, Q, out, in_, reads=(), writes=()):
        self._waits(Q, reads, writes)
        tgt = writes[0]
        if tgt.dsem is None:
            tgt.dsem = self.nc.alloc_semaphore("ds_" + tgt.name)
            self.dbufs.append(tgt)
        Q.e.dma_start(out=out, in_=in_).then_inc(tgt.dsem, 16)
        tgt.dcnt += 16
        self.evq += 1
        ev = (tgt.dsem, tgt.dcnt, "d_" + tgt.name)
        self._done(ev, reads, writes)

    def barrier(self):
        evs = [(E.sem, E.cnt, E.name) for E in self.engs if E.cnt > 0]
        evs += [(b.dsem, b.dcnt, "d_" + b.name) for b in self.dbufs if b.dcnt > 0]
        for E in self.engs:
            for ev in evs:
                if ev[2] == E.name:
                    continue
                E.wait(ev)

    def act(self, out, in_, func, reads, writes, **kw):
        return self.op(self.ACT, lambda: self.nc.scalar.activation(out=out, in_=in_, func=func, **kw), reads, writes)

    def dve(self, emit, reads, writes):
        return self.op(self.DVE, emit, reads, writes)


class Arena:
    def __init__(self, nc, base=16512, top=229344):
        self.nc = nc
        self.cur = base
        self.top = top
        self.n = 0

    def alloc(self, name, shape, dtype):
        nb = int(np.prod(shape[1:])) * (4 if dtype == F32 else 2)
        nb = (nb + 31) // 32 * 32
        self.n += 1
        t = self.nc.alloc_sbuf_tensor_at(f"{name}_{self.n}", list(shape), dtype, offset=self.cur)
        self.cur += nb
        assert self.cur <= self.top, f"SBUF overflow at {name}: {self.cur} > {self.top}"
        return t

    def mark(self):
        return self.cur

    def reset(self, m):
        self.cur = m


def build_program(debug=False):
    P = Prog(debug)
    nc = P.nc
    V = nc.vector
    offs, TOT = tile_table()

    xT = nc.dram_tensor("xT", [128, KC * EXT], F32, kind="ExternalInput").ap()
    wpk = nc.dram_tensor("wpk", [128, TOT], F32, kind="ExternalInput").ap()
    gains_d = nc.dram_tensor("gains", [128, 6 * KC], F32, kind="ExternalInput").ap()
    nab_d = nc.dram_tensor("nab", [128, 8 * 27 * 128], F32, kind="ExternalInput").ap()
    dlb_d = nc.dram_tensor("dlb", [128, 12 * 4 * 128], F32, kind="ExternalInput").ap()
    outT = nc.dram_tensor("outT", [128, KC * OWN], F32, kind="ExternalOutput").ap()
    xs1 = nc.dram_tensor("xs1", [128, KC * OWN], F32, kind="Internal").ap()
    xs2 = nc.dram_tensor("xs2", [128, KC * OWN], F32, kind="Internal").ap()
    dbg = {}
    if debug:
        dbg["h1"] = nc.dram_tensor("dbg_h1", [128, KC * EXT], BF16, kind="ExternalOutput").ap()
        dbg["oT"] = nc.dram_tensor("dbg_oT", [128, 10 * OWN], BF16, kind="ExternalOutput").ap()
    xT3 = xT.rearrange("p (k e) -> p k e", k=KC)
    xs13 = xs1.rearrange("p (k e) -> p k e", k=KC)
    xs23 = xs2.rearrange("p (k e) -> p k e", k=KC)
    outT3 = outT.rearrange("p (k e) -> p k e", k=KC)

    banks = []
    for i in range(8):
        t = nc.alloc_psum_tensor(f"psb{i}", [128, 512], F32)
        banks.append((t, P.buf(f"psb{i}")))
    bank_i = [0]

    def next_bank():
        b = banks[bank_i[0] % 8]
        bank_i[0] += 1
        return b

    AR = Arena(nc)
    ones = AR.alloc("ones", [128, 128], BF16)
    epst = AR.alloc("epst", [128, 1], F32)
    G = AR.alloc("G", [128, 6, KC], F32)
    Gh = AR.alloc("Gh", [128, 2, KC], F32)
    m_const = AR.mark()
    h1 = AR.alloc("h1", [128, KC, EXT], BF16)
    m_h1 = AR.mark()
    WS = {}

    def new_slots(n, size):
        WS["t"] = AR.alloc("wslots", [128, n, size], BF16)
        WS["b"] = P.bufs(n, f"slot{AR.n}_")
        WS["n"] = n
        WS["i"] = 0
    const_b = P.buf("const")
    G_b = P.buf("G")
    Gh_b = P.buf("Gh")

    P.op(P.DVE, lambda: V.memset(ones[:], 1.0), writes=[const_b])
    P.op(P.DVE, lambda: V.memset(epst[:], EPS), writes=[const_b])
    P.dma(P.SP, G[:].rearrange("p a k -> p (a k)"), gains_d[:, :], writes=[G_b])
    P.dve(lambda: V.tensor_scalar(out=Gh[:, 0, :], in0=G[:, 1, :], scalar1=0.5, scalar2=None, op0=ALU.mult),
          [G_b], [Gh_b])
    P.dve(lambda: V.tensor_scalar(out=Gh[:, 1, :], in0=G[:, 5, :], scalar1=0.5, scalar2=None, op0=ALU.mult),
          [G_b], [Gh_b])

    def wtile(name):
        off, n = offs[name]
        s = WS["i"] % WS["n"]
        WS["i"] += 1
        P.dma(P.POOL, WS["t"][:, s, 0:n], wpk[:, off:off + n], writes=[WS["b"][s]])
        return WS["t"][:, s, 0:n], WS["b"][s]

    def rms_stats(sq_aps, sq_bufs, std_t, std_b, rstd_t, rstd_b):
        bt, bb = next_bank()
        P.mm(bt[:, :], [(ones[:, :], a) for a in sq_aps], reads=list(sq_bufs) + [const_b], writes=[bb])
        P.act(std_t[:, :], bt[:, :], AF.Sqrt, [bb, const_b], [std_b], scale=1.0 / D, bias=epst[:, 0:1])
        P.dve(lambda: V.reciprocal(out=rstd_t[:, :], in_=std_t[:, :]), [std_b], [rstd_b])

    def ffn_phase(nblk, x_src, wpre, gi_pre, ghi, tail2):
        xb_t = [AR.alloc(f"xb{wpre}{i}", [128, KC, TB], F32) for i in range(2)]
        xb_b = [P.bufs(KC, f"xb{wpre}{i}_") for i in range(2)]
        h_t = AR.alloc(f"h{wpre}", [128, KC, TB], BF16)
        h_b = P.bufs(KC, f"h{wpre}_")
        sqf_t = AR.alloc(f"sqf{wpre}", [128, KC, TB], BF16)
        sqf_b = P.bufs(KC, f"sqf{wpre}_")
        a_t = AR.alloc(f"a{wpre}", [128, MC, TB], BF16)
        a_b = P.bufs(MC, f"a{wpre}_")
        f_t = AR.alloc(f"f{wpre}", [128, KC, TB], F32)
        f_b = P.bufs(KC, f"f{wpre}_")
        sg_t = [AR.alloc(f"sg{wpre}{i}", [128, TB], F32) for i in range(2)]
        sg_b = P.bufs(2, f"sg{wpre}_")
        tmp_t = [AR.alloc(f"tmp{wpre}{i}", [128, TB], F32) for i in range(2)]
        tmp_b = P.bufs(2, f"tmp{wpre}_")
        std1_t = AR.alloc(f"std{wpre}", [128, TB], F32)
        std1_b = P.buf(f"std{wpre}_")
        std_t, std_b = [std1_t] * 3, [std1_b] * 3
        rstd_t = [AR.alloc(f"rstd{wpre}{i}", [128, TB], F32) for i in range(3)]
        rstd_b = P.bufs(3, f"rstd{wpre}_")
        cnt = [0]

        def pre1(b):
            xt, xbb = xb_t[b % 2], xb_b[b % 2]
            P.dma(P.SP, xt[:, :, :], x_src(b), writes=xbb)
            P.act(h_t[:, :, :], xt[:, :, :], AF.Square, xbb, h_b)

        def pre2(b):
            xt, xbb = xb_t[b % 2], xb_b[b % 2]
            rms_stats([h_t[:, k, :] for k in range(KC)], h_b, std_t[0], std_b[0], rstd_t[0], rstd_b[0])
            for k in range(KC):
                P.dve(lambda k=k: V.scalar_tensor_tensor(
                    out=h_t[:, k, :], in0=xt[:, k, :], scalar=G[:, gi_pre, k:k + 1], in1=rstd_t[0][:, :],
                    op0=ALU.mult, op1=ALU.mult), [xbb[k], rstd_b[0], G_b], [h_b[k]])

        def gu(b, m):
            wt, wb = wtile_cached(f"f{wpre}gu{m // 2}")
            w4 = wt.rearrange("p (g k j) -> p g k j", g=2, k=KC)
            co = (m % 2) * 128
            (pg, pgb), (pu, pub) = next_bank(), next_bank()
            P.mm(pg[:, :], [(w4[:, 0, k, co:co + 128], h_t[:, k, :]) for k in range(KC)], [wb] + h_b, [pgb])
            P.mm(pu[:, :], [(w4[:, 1, k, co:co + 128], h_t[:, k, :]) for k in range(KC)], [wb] + h_b, [pub])
            i = cnt[0] % 2
            cnt[0] += 1
            P.act(sg_t[i][:, :], pg[:, :], AF.Silu, [pgb], [sg_b[i]])
            P.dve(lambda: V.tensor_tensor(out=a_t[:, m, :], in0=sg_t[i][:, :], in1=pu[:, :], op=ALU.mult),
                  [sg_b[i], pub], [a_b[m]])

        def down(b, dm):
            wt, wb = wtile_cached(f"f{wpre}d{dm // 2}")
            w3 = wt.rearrange("p (m j) -> p m j", m=MC)
            co = (dm % 2) * 128
            pf, pfb = next_bank()
            P.mm(pf[:, :], [(w3[:, m, co:co + 128], a_t[:, m, :]) for m in range(MC)], [wb] + a_b, [pfb])
            P.op(P.ACT, lambda: nc.scalar.copy(out=f_t[:, dm, :], in_=pf[:, :]), [pfb], [f_b[dm]])
            P.dve(lambda: V.tensor_tensor(out=sqf_t[:, dm, :], in0=pf[:, :], in1=f_t[:, dm, :], op=ALU.mult),
                  [pfb, f_b[dm]], [sqf_b[dm]])

        def tail1(b):
            xt, xbb = xb_t[b % 2], xb_b[b % 2]
            rms_stats([sqf_t[:, k, :] for k in range(KC)], sqf_b, std_t[1], std_b[1], rstd_t[1], rstd_b[1])
            for k in range(KC):
                i = k % 2
                P.dve(lambda k=k, i=i: V.scalar_tensor_tensor(
                    out=tmp_t[i][:, :], in0=f_t[:, k, :], scalar=Gh[:, ghi, k:k + 1], in1=rstd_t[1][:, :],
                    op0=ALU.mult, op1=ALU.mult), [f_b[k], rstd_b[1], Gh_b], [tmp_b[i]])
                P.dve(lambda k=k, i=i: V.tensor_tensor(out=xt[:, k, :], in0=xt[:, k, :], in1=tmp_t[i][:, :],
                                                      op=ALU.add), [xbb[k], tmp_b[i]], [xbb[k]])

        last_w = [None, None]

        def wtile_cached(name):
            if last_w[0] != name:
                last_w[0] = name
                last_w[1] = wtile(name)
            return last_w[1]

        ctx = dict(xb_t=xb_t, xb_b=xb_b, sqf_t=sqf_t, sqf_b=sqf_b, std_t=std_t, std_b=std_b,
                   rstd_t=rstd_t, rstd_b=rstd_b)
        pre1(0)
        pre2(0)
        for b in range(nblk):
            for m in range(MC):
                gu(b, m)
                if m == 1 and b > 0:
                    tail1(b - 1)
                    tail2(b - 1, ctx, 0)
                if m == 4 and b > 0:
                    tail2(b - 1, ctx, 1)
            if b + 1 < nblk:
                pre1(b + 1)
            down(b, 0)
            down(b, 1)
            if b + 1 < nblk:
                pre2(b + 1)
            for dm in range(2, KC):
                down(b, dm)
        tail1(nblk - 1)
        tail2(nblk - 1, ctx, 0)
        tail2(nblk - 1, ctx, 1)

    xs1_b = P.bufs(4, "xs1_")

    def tailA(b, c, part):
        xt, xbb = c["xb_t"][b % 2], c["xb_b"][b % 2]
        if part == 0:
            P.act(c["sqf_t"][:, :, :], xt[:, :, :], AF.Square, xbb, c["sqf_b"])
            if 2 <= b < 6:
                P.dma(P.SP, xs13[:, :, (b - 2) * TB:(b - 1) * TB], xt[:, :, :], reads=xbb, writes=[xs1_b[b - 2]])
        else:
            rms_stats([c["sqf_t"][:, k, :] for k in range(KC)], c["sqf_b"],
                      c["std_t"][2], c["std_b"][2], c["rstd_t"][2], c["rstd_b"][2])
            for k in range(KC):
                P.dve(lambda k=k: V.scalar_tensor_tensor(
                    out=h1[:, k, b * TB:(b + 1) * TB], in0=xt[:, k, :], scalar=G[:, 2, k:k + 1],
                    in1=c["rstd_t"][2][:, :], op0=ALU.mult, op1=ALU.mult),
                    [xbb[k], c["rstd_b"][2], G_b], [h1_b])

    h1_b = P.buf("h1")
    new_slots(3, 5632)
    ffn_phase(EXT // TB, lambda b: xT3[:, :, b * TB:(b + 1) * TB], 1, 0, 0, tailA)
    P.barrier()
    AR.reset(m_h1)

    if debug:
        db = P.buf("dbgh1")
        P.dma(P.SP, dbg["h1"][:, :], h1[:].rearrange("p k e -> p (k e)"), writes=[db])

    oT = AR.alloc("oT", [128, 10, OWN], BF16)
    m_oT = AR.mark()
    new_slots(2, 6144)
    QT = AR.alloc("QT", [128, 2, OWN], BF16)
    KT = AR.alloc("KT", [128, 2, EXT], BF16)
    VV = AR.alloc("VV", [128, 32, 256], BF16)
    sbs = [AR.alloc(f"sbs{i}", [128, 768], F32) for i in range(2)]
    PT = [AR.alloc(f"PT{i}", [128, 768], BF16) for i in range(2)]
    rden = [AR.alloc(f"rden{i}", [128, 128], F32) for i in range(2)]
    m_bias = AR.mark()
    nab = AR.alloc("nab", [128, 2, 16, 128], F32)
    AR.reset(m_bias)
    dlb = AR.alloc("dlb", [128, 4, 4, 128], F32)
    Stot = AR.alloc("Stot", [128, 2, OWN], F32)
    nab_d4 = nab_d.rearrange("p (h s q) -> p h s q", h=8, s=27)
    dlb_d4 = dlb_d.rearrange("p (h s q) -> p h s q", h=12, s=4)
    QT_b, KT_b, VV_b, nab_b, dlb_b = P.buf("QT"), P.buf("KT"), P.buf("VV"), P.buf("nab"), P.buf("dlb")
    Stot_b = P.bufs(2, "Stot")
    sbs_b, PT_b, rden_b = P.bufs(2, "sbs"), P.bufs(2, "PT"), P.bufs(2, "rden")
    oT_b = P.bufs(10, "oT")
    evac_i = [0]

    def evac(out, in_, reads, writes):
        evac_i[0] += 1
        if evac_i[0] % 2:
            P.op(P.ACT, lambda: nc.scalar.copy(out=out, in_=in_), reads, writes)
        else:
            P.dve(lambda: V.tensor_copy(out=out, in_=in_), reads, writes)

    def proj_fm(dst_t, dst_b, pp, w3, c0, e0, ntok):
        t0 = 0
        while t0 < ntok:
            n = min(512, ntok - t0)
            bt, bb = next_bank()
            P.mm(bt[:, 0:n], [(w3[:, k, c0:c0 + 128], h1[:, k, e0 + t0:e0 + t0 + n]) for k in range(KC)],
                 [cur_w[0]], [bb])
            evac(dst_t[:, pp, t0:t0 + n], bt[:, 0:n], [bb], [dst_b])
            t0 += n

    cur_w = [None]

    def attn_loop(steps, S_fn, PV_fn):
        for i in range(len(steps) + 1):
            if i < len(steps):
                S_fn(i, steps[i])
            if i > 0:
                PV_fn(i - 1, steps[i - 1])

    for hp in range(4):
        wt, wb = wtile(f"na{hp}")
        cur_w[0] = wb
        w3 = wt.rearrange("p (k j) -> p k j", k=KC)
        proj_fm(QT, QT_b, 0, w3, 0, HALO, OWN)
        proj_fm(KT, KT_b, 0, w3, 128, HALO - 256, 2560)
        for cb in range(5):
            bt, bb = next_bank()
            P._waits(P.PE, [wb], [bb, VV_b])
            inst = None
            for cc in range(4):
                cg = 4 * cb + cc
                e0 = HALO - 256 + 128 * cg
                for k in range(KC):
                    inst = nc.tensor.matmul(bt[:, cc * 128:(cc + 1) * 128], h1[:, k, e0:e0 + 128],
                                            w3[:, k, 256:384], start=(k == 0), stop=(k == KC - 1))
            ev = P.PE.signal(inst)
            P._done(ev, [wb], [bb])
            evac(VV[:, 4 * cb:4 * cb + 4, 0:128], bt[:, :].rearrange("p (c j) -> p c j", c=4), [bb], [VV_b])
        P.dma(P.SP, nab[:, :, 0:16, :], nab_d4[:, 2 * hp:2 * hp + 2, 0:16, :], writes=[nab_b])

        steps = [(j, hh) for j in range(16) for hh in range(2)]
        ud = {}

        def S_na(i, st, hp=hp):
            j, hh = st
            chunks, off = na_chunks(j)
            if j == 14 and hh == 0:
                P.dma(P.SP, nab[:, :, 0:11, :], nab_d4[:, 2 * hp:2 * hp + 2, 16:27, :], writes=[nab_b])
            if j >= 14:
                off -= 16
            nch = len(chunks)
            base = 64 * hh
            (bx, bxb), (by, byb) = next_bank(), next_bank()
            items = []
            for ci, cg in enumerate(chunks):
                tgt = (bx if ci < 4 else by)[:, (ci % 4) * 128:(ci % 4) * 128 + 128]
                items.append((tgt, KT[base:base + 64, 0, cg * 128:(cg + 1) * 128],
                              QT[base:base + 64, 0, j * 128:(j + 1) * 128]))
            P.mm_multi(items, [QT_b, KT_b], [bxb, byb])
            s = i % 2
            P.dve(lambda: V.scalar_tensor_tensor(
                out=sbs[s][:, 0:512], in0=bx[:, 0:512], scalar=0.125,
                in1=nab[:, hh, off:off + 4, :].rearrange("p c q -> p (c q)"), op0=ALU.mult, op1=ALU.add),
                [bxb, nab_b], [sbs_b[s]])
            n2 = (nch - 4) * 128
            P.dve(lambda: V.scalar_tensor_tensor(
                out=sbs[s][:, 512:512 + n2], in0=by[:, 0:n2], scalar=0.125,
                in1=nab[:, hh, off + 4:off + nch, :].rearrange("p c q -> p (c q)"), op0=ALU.mult, op1=ALU.add),
                [byb, nab_b], [sbs_b[s]])
            P.act(PT[s][:, 0:nch * 128], sbs[s][:, 0:nch * 128], AF.Exp, [sbs_b[s]], [PT_b[s]])

        def PV_na(i, st, hp=hp):
            j, hh = st
            chunks, off = na_chunks(j)
            base = 64 * hh
            s = i % 2
            if hh == 0:
                ud[j] = (next_bank(), next_bank())
            (bu, bub), (bd, bdb) = ud[j]
            P.mm(bu[base:base + 64, 0:128],
                 [(VV[:, cg, base:base + 64], PT[s][:, ci * 128:(ci + 1) * 128]) for ci, cg in enumerate(chunks)],
                 [PT_b[s], VV_b], [bub])
            P.mm(bd[base:base + 64, 0:128],
                 [(ones[:, 0:64], PT[s][:, ci * 128:(ci + 1) * 128]) for ci, cg in enumerate(chunks)],
                 [PT_b[s], const_b], [bdb])
            if hh == 1:
                r = j % 2
                P.dve(lambda: V.reciprocal(out=rden[r][:, :], in_=bd[:, 0:128]), [bdb], [rden_b[r]])
                P.dve(lambda: V.tensor_tensor(out=oT[:, hp, j * 128:(j + 1) * 128], in0=bu[:, 0:128],
                                              in1=rden[r][:, :], op=ALU.mult), [bub, rden_b[r]], [oT_b[hp]])

        attn_loop(steps, S_na, PV_na)

    P.barrier()
    for g in range(3):
        d = DIL[g]
        P.dma(P.SP, dlb[:, :, :, :], dlb_d4[:, 4 * g:4 * g + 4, :, :], writes=[dlb_b])
        wt, wb = wtile(f"dl{g}")
        cur_w[0] = wb
        w3 = wt.rearrange("p (k j) -> p k j", k=KC)
        span = OWN + 128 * d
        nbq = 16 // d
        nck = nbq + 1
        for pp in range(2):
            proj_fm(QT, QT_b, pp, w3, pp * 128, HALO, OWN)
            proj_fm(KT, KT_b, pp, w3, 256 + pp * 128, HALO - 64 * d, span)
        nchunks = d * nck
        for c0 in range(0, nchunks, 2):
            bt, bb = next_bank()
            P._waits(P.PE, [wb], [bb, VV_b])
            inst = None
            for cc in range(min(2, nchunks - c0)):
                cidx = c0 + cc
                r, ci = cidx // nck, cidx % nck
                e0 = HALO + r + d * (-64 + 128 * ci)
                for k in range(KC):
                    inst = nc.tensor.matmul(bt[:, cc * 256:(cc + 1) * 256],
                                            h1[:, k, ss(e0, d)], w3[:, k, 512:768],
                                            start=(k == 0), stop=(k == KC - 1))
            ev = P.PE.signal(inst)
            P._done(ev, [wb], [bb])
            ncc = min(2, nchunks - c0)
            evac(VV[:, c0:c0 + ncc, :], bt[:, 0:ncc * 256].rearrange("p (c j) -> p c j", c=ncc), [bb], [VV_b])

        steps = [(pp, r, blk, hh) for pp in range(2) for r in range(d) for blk in range(nbq) for hh in range(2)]
        ud = {}

        def S_dl(i, st, g=g, d=d, nbq=nbq):
            pp, r, blk, hh = st
            h = 4 * g + 2 * pp + hh
            base = 64 * hh
            bx, bxb = next_bank()
            q0 = r + 128 * d * blk
            items = []
            for c in range(2):
                k0 = r + 128 * d * (blk + c)
                items.append((bx[:, c * 128:(c + 1) * 128],
                              KT[base:base + 64, pp, ss(k0, d)],
                              QT[base:base + 64, pp, ss(q0, d)]))
            P.mm_multi(items, [QT_b, KT_b], [bxb])
            va = 1 if blk == 0 else 0
            vb = 3 if blk == nbq - 1 else 2
            s = i % 2
            P.dve(lambda: V.scalar_tensor_tensor(
                out=sbs[s][:, 0:256].rearrange("p (c q) -> p c q", c=2),
                in0=bx[:, 0:256].rearrange("p (c q) -> p c q", c=2), scalar=0.125,
                in1=dlb[:, 2 * pp + hh, va:vb + 1:vb - va, :], op0=ALU.mult, op1=ALU.add),
                [bxb, dlb_b], [sbs_b[s]])
            P.act(PT[s][:, 0:256], sbs[s][:, 0:256], AF.Exp, [sbs_b[s]], [PT_b[s]])

        def PV_dl(i, st, g=g, d=d, nck=nck):
            pp, r, blk, hh = st
            base = 64 * hh
            s = i % 2
            key = (pp, r, blk)
            if hh == 0:
                ud[key] = (next_bank(), next_bank())
            (bu, bub), (bd, bdb) = ud[key]
            vc = r * nck + blk
            P.mm(bu[base:base + 64, 0:128],
                 [(VV[:, vc + c, pp * 128 + base:pp * 128 + base + 64], PT[s][:, c * 128:(c + 1) * 128])
                  for c in range(2)], [PT_b[s], VV_b], [bub])
            P.mm(bd[base:base + 64, 0:128],
                 [(ones[:, 0:64], PT[s][:, c * 128:(c + 1) * 128]) for c in range(2)],
                 [PT_b[s], const_b], [bdb])
            if hh == 1:
                q0 = r + 128 * d * blk
                oc = 4 + 2 * g + pp
                P.op(P.ACT, lambda: nc.scalar.copy(out=oT[:, oc, ss(q0, d)], in_=bu[:, 0:128]),
                     [bub], [oT_b[oc]])
                if g == 0:
                    P.dve(lambda: V.tensor_copy(out=Stot[:, pp, ss(q0, d)], in_=bd[:, 0:128]),
                          [bdb], [Stot_b[pp]])
                else:
                    P.dve(lambda: V.tensor_tensor(out=Stot[:, pp, ss(q0, d)], in0=bd[:, 0:128],
                                                  in1=Stot[:, pp, ss(q0, d)], op=ALU.add),
                          [bdb, Stot_b[pp]], [Stot_b[pp]])

        attn_loop(steps, S_dl, PV_dl)

    for pp in range(2):
        P.dve(lambda: V.reciprocal(out=Stot[:, pp, :], in_=Stot[:, pp, :]), [Stot_b[pp]], [Stot_b[pp]])
        for g in range(3):
            oc = 4 + 2 * g + pp
            P.dve(lambda: V.tensor_tensor(out=oT[:, oc, :], in0=oT[:, oc, :], in1=Stot[:, pp, :], op=ALU.mult),
                  [oT_b[oc], Stot_b[pp]], [oT_b[oc]])
    P.barrier()
    if debug:
        db2 = P.buf("dbgoT")
        P.dma(P.SP, dbg["oT"][:, :], oT[:].rearrange("p k e -> p (k e)"), writes=[db2])

    AR.reset(m_oT)
    new_slots(2, 6656)
    xc_t = [AR.alloc("xc", [128, KC, TB], F32)] * 2
    xc_b = [P.bufs(KC, "xc_")] * 2
    mg_t = AR.alloc("mg", [128, KC, TB], BF16)
    mg_b = P.bufs(KC, "mg_")
    y_t = AR.alloc("y", [128, KC, TB], F32)
    y_b = P.bufs(KC, "y_")
    sqy_t = AR.alloc("sqy", [128, KC, TB], BF16)
    sqy_b = P.bufs(KC, "sqy_")
    sga_t = [AR.alloc(f"sga{i}", [128, TB], F32) for i in range(2)]
    sga_b = P.bufs(2, "sga_")
    t12_t = [AR.alloc(f"t12{i}", [128, TB], F32) for i in range(2)]
    t12_b = P.bufs(2, "t12_")
    stdc_t = AR.alloc("stdc", [128, TB], F32)
    rstdc_t = AR.alloc("rstdc", [128, TB], F32)
    stdc_b, rstdc_b = P.buf("stdc"), P.buf("rstdc")
    xs2_b = P.bufs(4, "xs2_")

    for tb in range(4):
        xt, xbb = xc_t[tb % 2], xc_b[tb % 2]
        P.dma(P.SP, xt[:, :, :], xs13[:, :, tb * TB:(tb + 1) * TB], writes=xbb)
        tok = slice(tb * TB, (tb + 1) * TB)
        etok = slice(HALO + tb * TB, HALO + (tb + 1) * TB)
        for dm in range(KC):
            if dm % 2 == 0:
                wt, wb = wtile(f"mixc{dm // 2}")
                w3 = wt.rearrange("p (k j) -> p k j", k=26)
            co = (dm % 2) * 128
            (pya, pyab), (pga, pgab), (pyb, pybb), (pgb, pgbb) = next_bank(), next_bank(), next_bank(), next_bank()
            P.mm(pya[:, :], [(w3[:, k, co:co + 128], oT[:, k, tok]) for k in range(4)], [wb] + oT_b[0:4], [pyab])
            P.mm(pga[:, :], [(w3[:, 10 + k, co:co + 128], h1[:, k, etok]) for k in range(KC)], [wb], [pgab])
            P.mm(pyb[:, :], [(w3[:, 4 + k, co:co + 128], oT[:, 4 + k, tok]) for k in range(6)],
                 [wb] + oT_b[4:10], [pybb])
            P.mm(pgb[:, :], [(w3[:, 18 + k, co:co + 128], h1[:, k, etok]) for k in range(KC)], [wb], [pgbb])
            P.act(sga_t[0][:, :], pga[:, :], AF.Sigmoid, [pgab], [sga_b[0]])
            P.dve(lambda: V.tensor_tensor(out=t12_t[0][:, :], in0=sga_t[0][:, :], in1=pya[:, :], op=ALU.mult),
                  [sga_b[0], pyab], [t12_b[0]])
            P.act(sga_t[1][:, :], pgb[:, :], AF.Sigmoid, [pgbb], [sga_b[1]])
            P.dve(lambda: V.tensor_tensor(out=t12_t[1][:, :], in0=sga_t[1][:, :], in1=pyb[:, :], op=ALU.mult),
                  [sga_b[1], pybb], [t12_b[1]])
            P.dve(lambda: V.tensor_tensor(out=mg_t[:, dm, :], in0=t12_t[0][:, :], in1=t12_t[1][:, :], op=ALU.add),
                  [t12_b[0], t12_b[1]], [mg_b[dm]])
        for dm in range(KC):
            if dm % 4 == 0:
                wt, wb = wtile(f"wout{dm // 4}")
                w3o = wt.rearrange("p (k j) -> p k j", k=KC)
            co = (dm % 4) * 128
            py, pyb_ = next_bank()
            P.mm(py[:, :], [(w3o[:, k, co:co + 128], mg_t[:, k, :]) for k in range(KC)], [wb] + mg_b, [pyb_])
            P.op(P.ACT, lambda: nc.scalar.copy(out=y_t[:, dm, :], in_=py[:, :]), [pyb_], [y_b[dm]])
            P.dve(lambda: V.tensor_tensor(out=sqy_t[:, dm, :], in0=py[:, :], in1=y_t[:, dm, :], op=ALU.mult),
                  [pyb_, y_b[dm]], [sqy_b[dm]])
        rms_stats([sqy_t[:, k, :] for k in range(KC)], sqy_b, stdc_t, stdc_b, rstdc_t, rstdc_b)
        for k in range(KC):
            i = k % 2
            P.dve(lambda: V.scalar_tensor_tensor(
                out=t12_t[i][:, :], in0=y_t[:, k, :], scalar=G[:, 3, k:k + 1], in1=rstdc_t[:, :],
                op0=ALU.mult, op1=ALU.mult), [y_b[k], rstdc_b, G_b], [t12_b[i]])
            P.dve(lambda: V.tensor_tensor(out=xt[:, k, :], in0=xt[:, k, :], in1=t12_t[i][:, :], op=ALU.add),
                  [xbb[k], t12_b[i]], [xbb[k]])
        P.dma(P.SP, xs23[:, :, tb * TB:(tb + 1) * TB], xt[:, :, :], reads=xbb, writes=[xs2_b[tb]])
    P.barrier()

    out_b = P.buf("out")

    def tailD(b, c, part):
        if part == 0:
            xt, xbb = c["xb_t"][b % 2], c["xb_b"][b % 2]
            P.dma(P.SP, outT3[:, :, b * TB:(b + 1) * TB], xt[:, :, :], reads=xbb, writes=[out_b])

    AR.reset(m_const)
    new_slots(5, 5632)
    ffn_phase(OWN // TB, lambda b: xs23[:, :, b * TB:(b + 1) * TB], 2, 4, 1, tailD)
    P.SP.wait((out_b.dsem, out_b.dcnt, "d_out"))
    P.barrier()
    return nc


_CACHE = {}


def prep_inputs(inp):
    x = np.asarray(inp["x"], np.float32)
    _, pk = weight_tiles({k: np.asarray(v, np.float32) for k, v in inp.items()})
    gl = ["ffn1_pre_g", "ffn1_post_g", "mix_pre_g", "mix_post_g", "ffn2_pre_g", "ffn2_post_g"]
    gains = np.stack([np.asarray(inp[n], np.float32)[0].reshape(KC, 128).T for n in gl], axis=1)
    gains = np.ascontiguousarray(gains).reshape(128, -1)
    rpb = np.asarray(inp["na_rpb"], np.float32)[0]
    in_maps = []
    for c in range(NCORES):
        b, qd = c // 4, c % 4
        lo = OWN * qd - HALO
        xe = np.zeros((EXT, D), np.float32)
        s0, s1 = max(lo, 0), min(lo + EXT, T)
        xe[s0 - lo:s1 - lo] = x[b, s0:s1]
        xTc = np.ascontiguousarray(xe.reshape(EXT, KC, 128).transpose(2, 1, 0)).reshape(128, -1)
        in_maps.append({"xT": xTc, "wpk": pk, "gains": gains,
                        "nab": na_bias_table(rpb, qd), "dlb": dil_bias_table(qd)})
    return in_maps


def kernel(**inputs):
    if "nc" not in _CACHE:
        _CACHE["nc"] = build_program(False)
    nc = _CACHE["nc"]
    in_maps = prep_inputs(inputs)
    res = run_bass_kernel_spmd(nc, in_maps, core_ids=list(range(NCORES)))
    out = np.empty((2, T, D), np.float32)
    for c in range(NCORES):
        b, qd = c // 4, c % 4
        o = np.asarray(res.results[c]["outT"]).reshape(128, KC, OWN)
        out[b, OWN * qd:OWN * (qd + 1)] = o.transpose(2, 1, 0).reshape(OWN, D)
    return out
```

```python
import numpy as np
import concourse.bass as bass
import concourse.mybir as mybir
from concourse.bass_utils import run_bass_kernel_spmd

F32 = mybir.dt.float32
BF16 = mybir.dt.bfloat16
AF = mybir.ActivationFunctionType
ALU = mybir.AluOpType

NCORES = 8
D = 1024
DFF = 2816
T = 8192
OWN = 2048
HALO = 1024
EXT = OWN + 2 * HALO
TB = 512
KC = D // 128
MC = DFF // 128
NEG = -30000.0
EPS = 1e-6
DIL = (1, 4, 16)
SLOT = 6656
NSLOT = 3
import os
SKIP = set(os.environ.get('KSKIP', '').split(','))

NA_OFF = {0: 0, 1: 6, 14: 16, 15: 21}


def ss(start, step, n=128):
    if step == 1:
        return slice(start, start + n)
    return slice(start, start + (n - 1) * step + 1, step)


def na_chunks(j):
    if j == 0:
        return list(range(0, 6)), 0
    if j == 1:
        return list(range(1, 6)), 6
    if j == 14:
        return list(range(14, 19)), 16
    if j == 15:
        return list(range(14, 20)), 21
    return list(range(j, j + 5)), 11


def _kmajor(w):
    K, n = w.shape
    return np.ascontiguousarray(w.reshape(K // 128, 128, n).transpose(1, 0, 2)).reshape(128, -1)


def weight_tiles(inp):
    tiles = []

    def add(name, arr):
        tiles.append((name, arr))

    for f in (1, 2):
        wg = inp[f"ffn{f}_w_gate"][0]
        wu = inp[f"ffn{f}_w_up"][0]
        wd = inp[f"ffn{f}_w_down"][0]
        for mg in range(MC // 2):
            cs = slice(mg * 256, (mg + 1) * 256)
            add(f"f{f}gu{mg}", np.concatenate([_kmajor(wg[:, cs]), _kmajor(wu[:, cs])], axis=1))
        for dg in range(4):
            cs = slice(dg * 256, (dg + 1) * 256)
            add(f"f{f}d{dg}", _kmajor(wd[:, cs]))
    win = inp["w_in"][0]
    add("nav", _kmajor(win[:, 1024:1536]))
    for hp in range(4):
        cols = np.concatenate([np.arange(hp * 128, hp * 128 + 128) + base for base in (0, 512)])
        add(f"naqk{hp}", _kmajor(win[:, cols]))
    for g in range(3):
        add(f"dlv{g}", _kmajor(win[:, 3072 + g * 256:3072 + (g + 1) * 256]))
        for pp in range(2):
            cols = np.concatenate([np.arange(g * 256 + pp * 128, g * 256 + pp * 128 + 128) + base
                                   for base in (1536, 2304)])
            add(f"dlqk{g}{pp}", _kmajor(win[:, cols]))
    wa = inp["w_branch_a"][0]
    wb = inp["w_branch_b"][0]
    for dg in range(4):
        cs = slice(dg * 256, (dg + 1) * 256)
        add(f"mixc{dg}", np.concatenate([
            _kmajor(wa[:, cs]), _kmajor(wb[:, cs]),
            _kmajor(win[:, 3840 + dg * 256: 3840 + (dg + 1) * 256]),
            _kmajor(win[:, 4864 + dg * 256: 4864 + (dg + 1) * 256])], axis=1))
    wo = inp["w_out"][0]
    for hf in range(2):
        add(f"wout{hf}", _kmajor(wo[:, hf * 512:(hf + 1) * 512]))
    offs = {}
    off = 0
    for name, arr in tiles:
        offs[name] = (off, arr.shape[1])
        off += arr.shape[1]
    pk = np.concatenate([a for _, a in tiles], axis=1).astype(np.float32)
    return offs, pk


def tile_table():
    offs = {}
    off = 0

    def add(name, n):
        nonlocal off
        offs[name] = (off, n)
        off += n

    for f in (1, 2):
        for mg in range(MC // 2):
            add(f"f{f}gu{mg}", 2 * KC * 256)
        for dg in range(4):
            add(f"f{f}d{dg}", MC * 256)
    add("nav", KC * 512)
    for hp in range(4):
        add(f"naqk{hp}", KC * 256)
    for g in range(3):
        add(f"dlv{g}", KC * 256)
        for pp in range(2):
            add(f"dlqk{g}{pp}", KC * 256)
    for dg in range(4):
        add(f"mixc{dg}", 26 * 256)
    for hf in range(2):
        add(f"wout{hf}", KC * 512)
    return offs, off


def na_bias_table(rpb, qd):
    out = np.full((128, 8, 27, 128), NEG, np.float32)
    i = np.arange(128)
    ki, kc = i // 64, i % 64
    qi, qc = i // 64, i % 64
    for j in (0, 1, 2, 14, 15):
        chunks, off = na_chunks(j)
        Rq = 32 * qd + 2 * j + qi
        row0 = np.clip(Rq - 4, 0, 120)
        col0 = np.clip(qc - 8, 0, 48)
        for ci, cg in enumerate(chunks):
            Rk = 32 * qd + 2 * cg - 4 + ki
            vr = (Rk[:, None] >= row0[None, :]) & (Rk[:, None] < row0[None, :] + 8) \
                & (Rk[:, None] >= 0) & (Rk[:, None] < 128)
            vc = (kc[:, None] >= col0[None, :]) & (kc[:, None] < col0[None, :] + 16)
            valid = vr & vc
            rr = np.clip(Rk[:, None] - Rq[None, :] + 7, 0, 14)
            cr = np.clip(kc[:, None] - qc[None, :], -15, 15) + 15
            vals = rpb[:, rr, cr]
            blk = np.where(valid[None], vals, np.float32(NEG)).astype(np.float32)
            out[:, :, off + ci, :] = blk.transpose(1, 0, 2)
    return out.reshape(128, -1)


def dil_bias_table(qd):
    out = np.empty((128, 6, 4, 2, 2, 128), np.float32)
    slopes = np.exp2(-8.0 * (np.arange(12, dtype=np.float32) + 1.0) / 12.0).astype(np.float32)
    i = np.arange(128)[:, None]
    q = np.arange(128)[None, :]
    for g in range(3):
        d = DIL[g]
        types = ["b"] * 4 if g == 2 else ["f", "m", "m", "l"]
        for pp in range(2):
            for hh in range(2):
                h = 4 * g + 2 * pp + hh
                for c, du in ((0, i - 64 - q), (1, i + 64 - q)):
                    dist = (d * np.abs(du)).astype(np.float32)
                    b = np.where(np.abs(du) <= 64, -(slopes[h] * dist), np.float32(NEG)).astype(np.float32)
                    for U, t in enumerate(types):
                        b2 = b.copy()
                        if c == 0 and t in ("f", "b") and qd == 0:
                            b2[:64, :] = NEG
                        if c == 1 and t in ("l", "b") and qd == 3:
                            b2[64:, :] = NEG
                        out[:, 2 * g + pp, U, hh, c, :] = b2
    return out.reshape(128, -1)


class Eng:
    def __init__(self, nc, e, name, own_wait):
        self.e = e
        self.name = name
        self.sem = nc.alloc_semaphore("es_" + name)
        self.cnt = 0
        self.seen = {}
        self.own_wait = own_wait

    def wait(self, ev):
        if ev is None:
            return
        sem, val, key = ev
        if key == self.name and not self.own_wait:
            return
        if self.seen.get(key, 0) >= val:
            return
        self.e.wait_ge(sem, val)
        self.seen[key] = val

    def signal(self, inst):
        self.cnt += 1
        inst.then_inc(self.sem, 1)
        return (self.sem, self.cnt, self.name)


class Buf:
    __slots__ = ("name", "w", "r", "dsem", "dcnt", "excl")

    def __init__(self, name):
        self.name = name
        self.excl = False
        self.w = None
        self.r = {}
        self.dsem = None
        self.dcnt = 0


class Prog:
    def __init__(self, debug=False):
        nc = bass.Bass("TRN2", target_bir_lowering=False)
        self.nc = nc
        self.debug = debug
        self.PE = Eng(nc, nc.tensor, "pe", False)
        self.ACT = Eng(nc, nc.scalar, "act", True)
        self.DVE = Eng(nc, nc.vector, "dve", True)
        self.SP = Eng(nc, nc.sync, "sp", False)
        self.POOL = Eng(nc, nc.gpsimd, "pool", False)
        self.engs = [self.PE, self.ACT, self.DVE, self.SP, self.POOL]
        self.dbufs = []
        self.nbuf = 0
        self.evq = 0

    def buf(self, name=None):
        self.nbuf += 1
        return Buf(name or f"b{self.nbuf}")

    def bufs(self, n, name=None):
        return [self.buf(f"{name}{i}" if name else None) for i in range(n)]

    def _waits(self, E, reads, writes):
        for b in reads:
            E.wait(b.w)
            if b.excl:
                for ev in b.r.values():
                    if ev[2] != E.name:
                        E.wait(ev)
        for b in writes:
            E.wait(b.w)
            for ev in b.r.values():
                E.wait(ev)

    def _done(self, ev, reads, writes):
        for b in reads:
            b.r[ev[2]] = ev
        for b in writes:
            b.w = ev
            b.r = {}

    def op(self, E, emit, reads=(), writes=()):
        self._waits(E, reads, writes)
        inst = emit()
        ev = E.signal(inst)
        self._done(ev, reads, writes)
        return inst

    def mm(self, out, pairs, reads=(), writes=()):
        self._waits(self.PE, reads, writes)
        n = len(pairs)
        inst = None
        for i, (l, r) in enumerate(pairs):
            inst = self.nc.tensor.matmul(out, l, r, start=(i == 0), stop=(i == n - 1))
        ev = self.PE.signal(inst)
        self._done(ev, reads, writes)

    def mm_multi(self, items, reads=(), writes=()):
        self._waits(self.PE, reads, writes)
        inst = None
        for (out, l, r) in items:
            inst = self.nc.tensor.matmul(out, l, r, start=True, stop=True)
        ev = self.PE.signal(inst)
        self._done(ev, reads, writes)

    def dma(self, Q, out, in_, reads=(), writes=()):
        self._waits(Q, reads, writes)
        tgt = writes[0]
        if tgt.dsem is None:
            tgt.dsem = self.nc.alloc_semaphore("ds_" + tgt.name)
            self.dbufs.append(tgt)
        Q.e.dma_start(out=out, in_=in_).then_inc(tgt.dsem, 16)
        tgt.dcnt += 16
        self.evq += 1
        ev = (tgt.dsem, tgt.dcnt, "d_" + tgt.name)
        self._done(ev, reads, writes)

    def barrier(self):
        evs = [(E.sem, E.cnt, E.name) for E in self.engs if E.cnt > 0]
        evs += [(b.dsem, b.dcnt, "d_" + b.name) for b in self.dbufs if b.dcnt > 0]
        for E in self.engs:
            for ev in evs:
                if ev[2] == E.name:
                    continue
                E.wait(ev)

    def act(self, out, in_, func, reads, writes, **kw):
        return self.op(self.ACT, lambda: self.nc.scalar.activation(out=out, in_=in_, func=func, **kw), reads, writes)

    def dve(self, emit, reads, writes):
        return self.op(self.DVE, emit, reads, writes)


class Arena:
    def __init__(self, nc, base=16512, top=229344):
        self.nc = nc
        self.cur = base
        self.top = top
        self.n = 0

    def alloc(self, name, shape, dtype):
        nb = int(np.prod(shape[1:])) * (4 if dtype == F32 else 2)
        nb = (nb + 31) // 32 * 32
        self.n += 1
        t = self.nc.alloc_sbuf_tensor_at(f"{name}_{self.n}", list(shape), dtype, offset=self.cur)
        self.cur += nb
        assert self.cur <= self.top, f"SBUF overflow at {name}: {self.cur} > {self.top}"
        return t

    def mark(self):
        return self.cur

    def reset(self, m):
        self.cur = m


def build_program(debug=False):
    P = Prog(debug)
    nc = P.nc
    V = nc.vector
    offs, TOT = tile_table()

    xT = nc.dram_tensor("xT", [128, KC * EXT], F32, kind="ExternalInput").ap()
    wpk = nc.dram_tensor("wpk", [128, TOT], F32, kind="ExternalInput").ap()
    gains_d = nc.dram_tensor("gains", [128, 6 * KC], F32, kind="ExternalInput").ap()
    nab_d = nc.dram_tensor("nab", [128, 8 * 27 * 128], F32, kind="ExternalInput").ap()
    dlb_d = nc.dram_tensor("dlb", [128, 6 * 4 * 512], F32, kind="ExternalInput").ap()
    outT = nc.dram_tensor("outT", [128, KC * OWN], F32, kind="ExternalOutput").ap()
    xs1 = nc.dram_tensor("xs1", [128, KC * OWN], F32, kind="Internal").ap()
    xs2 = nc.dram_tensor("xs2", [128, KC * OWN], F32, kind="Internal").ap()
    dbg = {}
    if debug:
        dbg["h1"] = nc.dram_tensor("dbg_h1", [128, KC * EXT], BF16, kind="ExternalOutput").ap()
        dbg["oT"] = nc.dram_tensor("dbg_oT", [128, 10 * OWN], BF16, kind="ExternalOutput").ap()
    xT3 = xT.rearrange("p (k e) -> p k e", k=KC)
    xs13 = xs1.rearrange("p (k e) -> p k e", k=KC)
    xs23 = xs2.rearrange("p (k e) -> p k e", k=KC)
    outT3 = outT.rearrange("p (k e) -> p k e", k=KC)

    ps_all = nc.alloc_psum_tensor("ps_all", [128, 8 * 512], F32)
    banks = [(ps_all[:, 512 * i:512 * (i + 1)], P.buf(f"psb{i}")) for i in range(8)]
    for _, bb_ in banks:
        bb_.excl = True
    bank_i = [0]

    def next_bank():
        b = banks[bank_i[0] % 8]
        bank_i[0] += 1
        return b

    AR = Arena(nc)
    ones = AR.alloc("ones", [128, 128], BF16)
    epst = AR.alloc("epst", [128, 1], F32)
    G = AR.alloc("G", [128, 6, KC], F32)
    Gh = AR.alloc("Gh", [128, 2, KC], F32)
    m_const = AR.mark()
    h1 = AR.alloc("h1", [128, KC, EXT], BF16)
    m_h1 = AR.mark()
    WS = {}

    def new_slots(n, size):
        WS["t"] = AR.alloc("wslots", [128, n, size], BF16)
        WS["b"] = P.bufs(n, f"slot{AR.n}_")
        WS["n"] = n
        WS["i"] = 0
    const_b = P.buf("const")
    G_b = P.buf("G")
    Gh_b = P.buf("Gh")

    P.op(P.DVE, lambda: V.memset(ones[:], 1.0), writes=[const_b])
    P.op(P.DVE, lambda: V.memset(epst[:], EPS), writes=[const_b])
    P.dma(P.SP, G[:].rearrange("p a k -> p (a k)"), gains_d[:, :], writes=[G_b])
    P.dve(lambda: V.tensor_scalar(out=Gh[:, 0, :], in0=G[:, 1, :], scalar1=0.5, scalar2=None, op0=ALU.mult),
          [G_b], [Gh_b])
    P.dve(lambda: V.tensor_scalar(out=Gh[:, 1, :], in0=G[:, 5, :], scalar1=0.5, scalar2=None, op0=ALU.mult),
          [G_b], [Gh_b])

    def wtile(name):
        off, n = offs[name]
        s = WS["i"] % WS["n"]
        WS["i"] += 1
        P.dma(P.POOL, WS["t"][:, s, 0:n], wpk[:, off:off + n], writes=[WS["b"][s]])
        return WS["t"][:, s, 0:n], WS["b"][s]

    def rms_stats(sq_aps, sq_bufs, std_t, std_b, rstd_t, rstd_b):
        bt, bb = next_bank()
        P.mm(bt[:, :], [(ones[:, :], a) for a in sq_aps], reads=list(sq_bufs) + [const_b], writes=[bb])
        P.act(std_t[:, :], bt[:, :], AF.Sqrt, [bb, const_b], [std_b], scale=1.0 / D, bias=epst[:, 0:1])
        P.dve(lambda: V.reciprocal(out=rstd_t[:, :], in_=std_t[:, :]), [std_b], [rstd_b])

    def ffn_phase(nblk, x_src, wpre, gi_pre, ghi, tail2):
        xb_t = [AR.alloc(f"xb{wpre}{i}", [128, KC, TB], F32) for i in range(2)]
        xb_b = [P.bufs(KC, f"xb{wpre}{i}_") for i in range(2)]
        h_t = AR.alloc(f"h{wpre}", [128, KC, TB], BF16)
        h_b = P.bufs(KC, f"h{wpre}_")
        sqf_t = AR.alloc(f"sqf{wpre}", [128, KC, TB], BF16)
        sqf_b = P.bufs(KC, f"sqf{wpre}_")
        a_t = AR.alloc(f"a{wpre}", [128, MC, TB], BF16)
        a_b = P.bufs(MC, f"a{wpre}_")
        f_t = AR.alloc(f"f{wpre}", [128, KC, TB], F32)
        f_b = P.bufs(KC, f"f{wpre}_")
        sg_t = [AR.alloc(f"sg{wpre}{i}", [128, TB], F32) for i in range(2)]
        sg_b = P.bufs(2, f"sg{wpre}_")
        tmp_t = [AR.alloc(f"tmp{wpre}{i}", [128, TB], F32) for i in range(2)]
        tmp_b = P.bufs(2, f"tmp{wpre}_")
        std1_t = AR.alloc(f"std{wpre}", [128, TB], F32)
        std1_b = P.buf(f"std{wpre}_")
        std_t, std_b = [std1_t] * 3, [std1_b] * 3
        rstd_t = [AR.alloc(f"rstd{wpre}{i}", [128, TB], F32) for i in range(3)]
        rstd_b = P.bufs(3, f"rstd{wpre}_")
        cnt = [0]

        def pre1(b):
            xt, xbb = xb_t[b % 2], xb_b[b % 2]
            P.dma(P.SP, xt[:, :, :], x_src(b), writes=xbb)
            P.act(h_t[:, :, :], xt[:, :, :], AF.Square, xbb, h_b)

        def pre2(b):
            xt, xbb = xb_t[b % 2], xb_b[b % 2]
            rms_stats([h_t[:, k, :] for k in range(KC)], h_b, std_t[0], std_b[0], rstd_t[0], rstd_b[0])
            for k in range(KC):
                P.dve(lambda k=k: V.scalar_tensor_tensor(
                    out=h_t[:, k, :], in0=xt[:, k, :], scalar=G[:, gi_pre, k:k + 1], in1=rstd_t[0][:, :],
                    op0=ALU.mult, op1=ALU.mult), [xbb[k], rstd_b[0], G_b], [h_b[k]])

        def gu(b, m):
            wt, wb = wtile_cached(f"f{wpre}gu{m // 2}")
            w4 = wt.rearrange("p (g k j) -> p g k j", g=2, k=KC)
            co = (m % 2) * 128
            (pg, pgb), (pu, pub) = next_bank(), next_bank()
            P.mm(pg[:, :], [(w4[:, 0, k, co:co + 128], h_t[:, k, :]) for k in range(KC)], [wb] + h_b, [pgb])
            P.mm(pu[:, :], [(w4[:, 1, k, co:co + 128], h_t[:, k, :]) for k in range(KC)], [wb] + h_b, [pub])
            i = cnt[0] % 2
            cnt[0] += 1
            P.act(sg_t[i][:, :], pg[:, :], AF.Silu, [pgb], [sg_b[i]])
            P.dve(lambda: V.tensor_tensor(out=a_t[:, m, :], in0=sg_t[i][:, :], in1=pu[:, :], op=ALU.mult),
                  [sg_b[i], pub], [a_b[m]])

        def down(b, dm):
            wt, wb = wtile_cached(f"f{wpre}d{dm // 2}")
            w3 = wt.rearrange("p (m j) -> p m j", m=MC)
            co = (dm % 2) * 128
            pf, pfb = next_bank()
            P.mm(pf[:, :], [(w3[:, m, co:co + 128], a_t[:, m, :]) for m in range(MC)], [wb] + a_b, [pfb])
            P.op(P.ACT, lambda: nc.scalar.copy(out=f_t[:, dm, :], in_=pf[:, :]), [pfb], [f_b[dm]])
            P.dve(lambda: V.tensor_tensor(out=sqf_t[:, dm, :], in0=pf[:, :], in1=f_t[:, dm, :], op=ALU.mult),
                  [pfb, f_b[dm]], [sqf_b[dm]])

        def tail1(b):
            xt, xbb = xb_t[b % 2], xb_b[b % 2]
            rms_stats([sqf_t[:, k, :] for k in range(KC)], sqf_b, std_t[1], std_b[1], rstd_t[1], rstd_b[1])
            for k in range(KC):
                i = k % 2
                P.dve(lambda k=k, i=i: V.scalar_tensor_tensor(
                    out=tmp_t[i][:, :], in0=f_t[:, k, :], scalar=Gh[:, ghi, k:k + 1], in1=rstd_t[1][:, :],
                    op0=ALU.mult, op1=ALU.mult), [f_b[k], rstd_b[1], Gh_b], [tmp_b[i]])
                P.dve(lambda k=k, i=i: V.tensor_tensor(out=xt[:, k, :], in0=xt[:, k, :], in1=tmp_t[i][:, :],
                                                      op=ALU.add), [xbb[k], tmp_b[i]], [xbb[k]])

        last_w = [None, None]

        def wtile_cached(name):
            if last_w[0] != name:
                last_w[0] = name
                last_w[1] = wtile(name)
            return last_w[1]

        ctx = dict(xb_t=xb_t, xb_b=xb_b, sqf_t=sqf_t, sqf_b=sqf_b, std_t=std_t, std_b=std_b,
                   rstd_t=rstd_t, rstd_b=rstd_b)
        pre1(0)
        pre2(0)
        for b in range(nblk):
            for m in range(MC):
                gu(b, m)
                if m == 1 and b > 0:
                    tail1(b - 1)
                    tail2(b - 1, ctx, 0)
                if m == 4 and b > 0:
                    tail2(b - 1, ctx, 1)
            if b + 1 < nblk:
                pre1(b + 1)
            down(b, 0)
            down(b, 1)
            if b + 1 < nblk:
                pre2(b + 1)
            for dm in range(2, KC):
                down(b, dm)
        tail1(nblk - 1)
        tail2(nblk - 1, ctx, 0)
        tail2(nblk - 1, ctx, 1)

    xs1_b = P.bufs(4, "xs1_")

    def tailA(b, c, part):
        xt, xbb = c["xb_t"][b % 2], c["xb_b"][b % 2]
        if part == 0:
            P.act(c["sqf_t"][:, :, :], xt[:, :, :], AF.Square, xbb, c["sqf_b"])
            if 2 <= b < 6:
                P.dma(P.SP, xs13[:, :, (b - 2) * TB:(b - 1) * TB], xt[:, :, :], reads=xbb, writes=[xs1_b[b - 2]])
        else:
            rms_stats([c["sqf_t"][:, k, :] for k in range(KC)], c["sqf_b"],
                      c["std_t"][2], c["std_b"][2], c["rstd_t"][2], c["rstd_b"][2])
            for k in range(KC):
                P.dve(lambda k=k: V.scalar_tensor_tensor(
                    out=h1[:, k, b * TB:(b + 1) * TB], in0=xt[:, k, :], scalar=G[:, 2, k:k + 1],
                    in1=c["rstd_t"][2][:, :], op0=ALU.mult, op1=ALU.mult),
                    [xbb[k], c["rstd_b"][2], G_b], [h1_b])

    h1_b = P.buf("h1")
    new_slots(3, 5632)
    ffn_phase(EXT // TB, lambda b: xT3[:, :, b * TB:(b + 1) * TB], 1, 0, 0, tailA)
    P.barrier()
    AR.reset(m_h1)

    if debug:
        db = P.buf("dbgh1")
        P.dma(P.SP, dbg["h1"][:, :], h1[:].rearrange("p k e -> p (k e)"), writes=[db])

    oT = AR.alloc("oT", [128, 10, OWN], BF16)
    m_oT = AR.mark()
    new_slots(3, 4096)
    QT = AR.alloc("QT", [128, 2, OWN], BF16)
    KT = AR.alloc("KT", [128, EXT], BF16)
    VVf = AR.alloc("VV", [128, 20 * 512], BF16)
    PTf = [AR.alloc(f"PT{i}", [128, 1536], BF16) for i in range(2)]
    sbs = [AR.alloc(f"sbs{i}", [128, 1536], F32) for i in range(2)]
    sbs_b = P.bufs(2, "sbs")
    m_bias = AR.mark()
    nab = AR.alloc("nab", [128, 2, 16, 128], F32)
    DenSB = AR.alloc("DenSB", [128, OWN], F32)
    AR.reset(m_bias)
    dlb = AR.alloc("dlb", [128, 4, 512], F32)
    Stot = AR.alloc("Stot", [128, 2, OWN], F32)
    nab_d4 = nab_d.rearrange("p (h s q) -> p h s q", h=8, s=27)
    dlb_d3 = dlb_d.rearrange("p (t u x) -> p t u x", t=6, u=4)
    QT_b, KT_b, VV_b, nab_b, dlb_b, Den_b = (P.buf("QT"), P.buf("KT"), P.buf("VV"), P.buf("nab"),
                                             P.buf("dlb"), P.buf("Den"))
    Stot_b = P.bufs(2, "Stot")
    PT_b = P.bufs(2, "PT")
    oT_b = P.bufs(10, "oT")
    evac_i = [0]

    def evac(out, in_, reads, writes):
        evac_i[0] += 1
        if evac_i[0] % 2:
            P.op(P.ACT, lambda: nc.scalar.copy(out=out, in_=in_), reads, writes)
        else:
            P.dve(lambda: V.tensor_copy(out=out, in_=in_), reads, writes)

    P.dve(lambda: V.memset(QT[64:128, 0, :], 0.0), [], [QT_b])
    P.dve(lambda: V.memset(QT[0:64, 1, :], 0.0), [], [QT_b])

    def proj_q(w3, wb):
        for tb in range(OWN // 512):
            bt, bb = next_bank()
            P.mm(bt[:, :], [(w3[:, k, 0:128], h1[:, k, HALO + tb * 512:HALO + (tb + 1) * 512]) for k in range(KC)],
                 [wb], [bb])
            P.op(P.ACT, lambda: nc.scalar.copy(out=QT[0:64, 0, tb * 512:(tb + 1) * 512], in_=bt[0:64, :]),
                 [bb], [QT_b])
            P.dve(lambda: V.tensor_copy(out=QT[64:128, 1, tb * 512:(tb + 1) * 512], in_=bt[64:128, :]),
                  [bb], [QT_b])

    def proj_fm(dst_t, dst_b, w3, c0, e0, ntok, wb):
        t0 = 0
        while t0 < ntok:
            n = min(512, ntok - t0)
            bt, bb = next_bank()
            P.mm(bt[:, 0:n], [(w3[:, k, c0:c0 + 128], h1[:, k, e0 + t0:e0 + t0 + n]) for k in range(KC)],
                 [wb], [bb])
            evac(dst_t[:, t0:t0 + n], bt[:, 0:n], [bb], [dst_b])
            t0 += n

    def vproj(e_of_chunk, nchunks, w3, nfeat, wb, d):
        VVv = VVf[:, 0:nchunks * nfeat].rearrange("p (c f) -> p c f", f=nfeat)
        per = 512 // nfeat
        for c0 in range(0, nchunks, per):
            ncc = min(per, nchunks - c0)
            bt, bb = next_bank()
            P._waits(P.PE, [wb], [bb])
            inst = None
            for cc in range(ncc):
                e0 = e_of_chunk(c0 + cc)
                for k in range(KC):
                    inst = nc.tensor.matmul(bt[:, cc * nfeat:(cc + 1) * nfeat], h1[:, k, ss(e0, d)],
                                            w3[:, k, 0:nfeat], start=(k == 0), stop=(k == KC - 1))
            ev = P.PE.signal(inst)
            P._done(ev, [wb], [bb])
            evac(VVv[:, c0:c0 + ncc, :], bt[:, 0:ncc * nfeat].rearrange("p (c f) -> p c f", f=nfeat),
                 [bb], [VV_b])
        return VVv

    def attn_loop(steps, S_fn, PV_fn):
        for i in range(len(steps) + 1):
            if i < len(steps):
                S_fn(i, steps[i])
            if i > 0 and 'pv' not in SKIP:
                PV_fn(i - 1, steps[i - 1])

    wt, wb = wtile("nav")
    VVn = vproj(lambda cg: HALO - 256 + 128 * cg, 20, wt.rearrange("p (k j) -> p k j", k=KC), 512, wb, 1)
    for hp in range(4):
        wt, wb = wtile(f"naqk{hp}")
        w3 = wt.rearrange("p (k j) -> p k j", k=KC)
        proj_q(w3, wb)
        proj_fm(KT, KT_b, w3, 128, HALO - 256, 2560, wb)
        P.dma(P.SP, nab[:, :, 0:16, :], nab_d4[:, 2 * hp:2 * hp + 2, 0:16, :], writes=[nab_b])

        def S_na(i, j, hp=hp):
            chunks, off = na_chunks(j)
            nch = len(chunks)
            s = i % 2
            n = nch * 128
            st = ps_all[:, 1536 * s:1536 * s + 2 * n].rearrange("p (h c) -> p h c", h=2)
            sb = [banks[3 * s + t][1] for t in range(3)]
            items = []
            for hh in range(2):
                for ci, cg in enumerate(chunks):
                    items.append((st[:, hh, ci * 128:(ci + 1) * 128],
                                  KT[:, cg * 128:(cg + 1) * 128],
                                  QT[:, hh, j * 128:(j + 1) * 128]))
            P.mm_multi(items, [QT_b, KT_b], sb)
            if j == 14:
                P.dma(P.SP, nab[:, :, 0:11, :], nab_d4[:, 2 * hp:2 * hp + 2, 16:27, :], writes=[nab_b])
            if j >= 14:
                off -= 16
            sv = sbs[s][:, 0:2 * n].rearrange("p (h c) -> p h c", h=2)
            P.dve(lambda: V.scalar_tensor_tensor(
                out=sv, in0=st, scalar=0.125,
                in1=nab[:, :, off:off + nch, :].rearrange("p h c q -> p h (c q)"),
                op0=ALU.mult, op1=ALU.add), sb + [nab_b], [sbs_b[s]])
            P.act(PTf[s][:, 0:2 * n], sbs[s][:, 0:2 * n], AF.Exp, [sbs_b[s]], [PT_b[s]])

        def PV_na(i, j, hp=hp):
            chunks, off = na_chunks(j)
            s = i % 2
            PTv = PTf[s][:, 0:2 * len(chunks) * 128].rearrange("p (h c) -> p h c", h=2)
            ud, udb = banks[6]
            dd, ddb = banks[7]
            for hh in range(2):
                P.mm(ud[64 * hh:64 * hh + 64, 0:128],
                     [(VVn[:, cg, hp * 128 + 64 * hh:hp * 128 + 64 * hh + 64], PTv[:, hh, ci * 128:(ci + 1) * 128])
                      for ci, cg in enumerate(chunks)], [PT_b[s], VV_b], [udb])
            for hh in range(2):
                P.mm(dd[64 * hh:64 * hh + 64, 0:128],
                     [(ones[:, 0:64], PTv[:, hh, ci * 128:(ci + 1) * 128]) for ci, cg in enumerate(chunks)],
                     [PT_b[s], const_b], [ddb])
            P.op(P.ACT, lambda: nc.scalar.copy(out=oT[:, hp, j * 128:(j + 1) * 128], in_=ud[:, 0:128]),
                 [udb], [oT_b[hp]])
            P.dve(lambda: V.tensor_copy(out=DenSB[:, j * 128:(j + 1) * 128], in_=dd[:, 0:128]),
                  [ddb], [Den_b])

        if 'na' not in SKIP:
            attn_loop(list(range(16)), S_na, PV_na)
        P.dve(lambda: V.reciprocal(out=DenSB[:, :], in_=DenSB[:, :]), [Den_b], [Den_b])
        P.dve(lambda: V.tensor_tensor(out=oT[:, hp, :], in0=oT[:, hp, :], in1=DenSB[:, :], op=ALU.mult),
              [oT_b[hp], Den_b], [oT_b[hp]])

    P.barrier()
    for g in range(3):
        d = DIL[g]
        span = OWN + 128 * d
        nbq = 16 // d
        nck = nbq + 1
        wt, wb = wtile(f"dlv{g}")
        VVd = vproj(lambda cidx: HALO + (cidx // nck) + d * (-64 + 128 * (cidx % nck)), d * nck,
                    wt.rearrange("p (k j) -> p k j", k=KC), 256, wb, d)
        for pp in range(2):
            wt, wb = wtile(f"dlqk{g}{pp}")
            w3 = wt.rearrange("p (k j) -> p k j", k=KC)
            proj_q(w3, wb)
            proj_fm(KT, KT_b, w3, 128, HALO - 64 * d, span, wb)
            P.dma(P.SP, dlb[:, :, :], dlb_d3[:, 2 * g + pp, :, :], writes=[dlb_b])
            oc = 4 + 2 * g + pp
            if d == 16:
                steps = [((r, 0), (r + 1, 0), 0) for r in range(0, 16, 2)]
            else:
                steps = []
                for r in range(d):
                    for b0 in range(0, nbq, 2):
                        uoff = 0 if b0 == 0 else (2 if b0 == nbq - 2 else 1)
                        steps.append(((r, b0), (r, b0 + 1), uoff))

            def S_dl(i, stp, d=d):
                u0, u1, uoff = stp
                s = i % 2
                flat = ps_all[:, 1024 * s:1024 * (s + 1)]
                st = flat.rearrange("p (u h c q) -> p u h c q", u=2, h=2, c=2)
                sb = [banks[2 * s][1], banks[2 * s + 1][1]]
                items = []
                for u, (r, blk) in enumerate((u0, u1)):
                    for hh in range(2):
                        for c in range(2):
                            items.append((st[:, u, hh, c, :],
                                          KT[:, ss(r + 128 * d * (blk + c), d)],
                                          QT[:, hh, ss(r + 128 * d * blk, d)]))
                P.mm_multi(items, [QT_b, KT_b], sb)
                P.dve(lambda: V.scalar_tensor_tensor(
                    out=sbs[s][:, 0:1024], in0=flat, scalar=0.125,
                    in1=dlb[:, uoff:uoff + 2, :].rearrange("p u x -> p (u x)"),
                    op0=ALU.mult, op1=ALU.add), sb + [dlb_b], [sbs_b[s]])
                P.act(PTf[s][:, 0:1024], sbs[s][:, 0:1024], AF.Exp, [sbs_b[s]], [PT_b[s]])

            def PV_dl(i, stp, d=d, g=g, pp=pp, oc=oc, nck=nck):
                if 'pvdl' in SKIP:
                    return
                u0, u1, uoff = stp
                s = i % 2
                PTv = PTf[s][:, 0:1024].rearrange("p (u h c q) -> p u h c q", u=2, h=2, c=2)
                ub, ubb = banks[4 + 2 * (i % 2)]
                db_, dbb = banks[5 + 2 * (i % 2)]
                for u, (r, blk) in enumerate((u0, u1)):
                    vc = r * nck + blk
                    for hh in range(2):
                        P.mm(ub[64 * hh:64 * hh + 64, u * 128:(u + 1) * 128],
                             [(VVd[:, vc + c, pp * 128 + 64 * hh:pp * 128 + 64 * hh + 64], PTv[:, u, hh, c, :])
                              for c in range(2)], [PT_b[s], VV_b], [ubb])
                for u in range(2):
                    for hh in range(2):
                        P.mm(db_[64 * hh:64 * hh + 64, u * 128:(u + 1) * 128],
                             [(ones[:, 0:64], PTv[:, u, hh, c, :]) for c in range(2)],
                             [PT_b[s], const_b], [dbb])
                if d == 16:
                    pieces = [(oT[:, oc, ss(u_[0], 16)], Stot[:, pp, ss(u_[0], 16)],
                               ub[:, k_ * 128:(k_ + 1) * 128], db_[:, k_ * 128:(k_ + 1) * 128])
                              for k_, u_ in enumerate((u0, u1))]
                else:
                    q0 = u0[0] + 128 * d * u0[1]
                    pieces = [(oT[:, oc, ss(q0, d, 256)], Stot[:, pp, ss(q0, d, 256)], ub[:, 0:256], db_[:, 0:256])]
                for (o_dst, s_dst, u_src, d_src) in pieces:
                    P.op(P.ACT, lambda: nc.scalar.copy(out=o_dst, in_=u_src), [ubb], [oT_b[oc]])
                    if g == 0:
                        P.dve(lambda: V.tensor_copy(out=s_dst, in_=d_src), [dbb], [Stot_b[pp]])
                    else:
                        P.dve(lambda: V.tensor_tensor(out=s_dst, in0=d_src, in1=s_dst, op=ALU.add),
                              [dbb, Stot_b[pp]], [Stot_b[pp]])

            if f'dl{g}' not in SKIP:
                attn_loop(steps, S_dl, PV_dl)

    for pp in range(2):
        P.dve(lambda: V.reciprocal(out=Stot[:, pp, :], in_=Stot[:, pp, :]), [Stot_b[pp]], [Stot_b[pp]])
        for g in range(3):
            oc = 4 + 2 * g + pp
            P.dve(lambda: V.tensor_tensor(out=oT[:, oc, :], in0=oT[:, oc, :], in1=Stot[:, pp, :], op=ALU.mult),
                  [oT_b[oc], Stot_b[pp]], [oT_b[oc]])
    P.barrier()
    if debug:
        db2 = P.buf("dbgoT")
        P.dma(P.SP, dbg["oT"][:, :], oT[:].rearrange("p k e -> p (k e)"), writes=[db2])

    AR.reset(m_oT)
    new_slots(2, 6656)
    xc_t = [AR.alloc("xc", [128, KC, TB], F32)] * 2
    xc_b = [P.bufs(KC, "xc_")] * 2
    mg_t = AR.alloc("mg", [128, KC, TB], BF16)
    mg_b = P.bufs(KC, "mg_")
    y_t = AR.alloc("y", [128, KC, TB], F32)
    y_b = P.bufs(KC, "y_")
    sqy_t = AR.alloc("sqy", [128, KC, TB], BF16)
    sqy_b = P.bufs(KC, "sqy_")
    sga_t = [AR.alloc(f"sga{i}", [128, TB], F32) for i in range(2)]
    sga_b = P.bufs(2, "sga_")
    t12_t = [AR.alloc(f"t12{i}", [128, TB], F32) for i in range(2)]
    t12_b = P.bufs(2, "t12_")
    stdc_t = AR.alloc("stdc", [128, TB], F32)
    rstdc_t = AR.alloc("rstdc", [128, TB], F32)
    stdc_b, rstdc_b = P.buf("stdc"), P.buf("rstdc")
    xs2_b = P.bufs(4, "xs2_")

    for tb in range(4):
        xt, xbb = xc_t[tb % 2], xc_b[tb % 2]
        P.dma(P.SP, xt[:, :, :], xs13[:, :, tb * TB:(tb + 1) * TB], writes=xbb)
        tok = slice(tb * TB, (tb + 1) * TB)
        etok = slice(HALO + tb * TB, HALO + (tb + 1) * TB)
        for dm in range(KC):
            if dm % 2 == 0:
                wt, wb = wtile(f"mixc{dm // 2}")
                w3 = wt.rearrange("p (k j) -> p k j", k=26)
            co = (dm % 2) * 128
            (pya, pyab), (pga, pgab), (pyb, pybb), (pgb, pgbb) = next_bank(), next_bank(), next_bank(), next_bank()
            P.mm(pya[:, :], [(w3[:, k, co:co + 128], oT[:, k, tok]) for k in range(4)], [wb] + oT_b[0:4], [pyab])
            P.mm(pga[:, :], [(w3[:, 10 + k, co:co + 128], h1[:, k, etok]) for k in range(KC)], [wb], [pgab])
            P.mm(pyb[:, :], [(w3[:, 4 + k, co:co + 128], oT[:, 4 + k, tok]) for k in range(6)],
                 [wb] + oT_b[4:10], [pybb])
            P.mm(pgb[:, :], [(w3[:, 18 + k, co:co + 128], h1[:, k, etok]) for k in range(KC)], [wb], [pgbb])
            P.act(sga_t[0][:, :], pga[:, :], AF.Sigmoid, [pgab], [sga_b[0]])
            P.dve(lambda: V.tensor_tensor(out=t12_t[0][:, :], in0=sga_t[0][:, :], in1=pya[:, :], op=ALU.mult),
                  [sga_b[0], pyab], [t12_b[0]])
            P.act(sga_t[1][:, :], pgb[:, :], AF.Sigmoid, [pgbb], [sga_b[1]])
            P.dve(lambda: V.tensor_tensor(out=t12_t[1][:, :], in0=sga_t[1][:, :], in1=pyb[:, :], op=ALU.mult),
                  [sga_b[1], pybb], [t12_b[1]])
            P.dve(lambda: V.tensor_tensor(out=mg_t[:, dm, :], in0=t12_t[0][:, :], in1=t12_t[1][:, :], op=ALU.add),
                  [t12_b[0], t12_b[1]], [mg_b[dm]])
        for dm in range(KC):
            if dm % 4 == 0:
                wt, wb = wtile(f"wout{dm // 4}")
                w3o = wt.rearrange("p (k j) -> p k j", k=KC)
            co = (dm % 4) * 128
            py, pyb_ = next_bank()
            P.mm(py[:, :], [(w3o[:, k, co:co + 128], mg_t[:, k, :]) for k in range(KC)], [wb] + mg_b, [pyb_])
            P.op(P.ACT, lambda: nc.scalar.copy(out=y_t[:, dm, :], in_=py[:, :]), [pyb_], [y_b[dm]])
            P.dve(lambda: V.tensor_tensor(out=sqy_t[:, dm, :], in0=py[:, :], in1=y_t[:, dm, :], op=ALU.mult),
                  [pyb_, y_b[dm]], [sqy_b[dm]])
        rms_stats([sqy_t[:, k, :] for k in range(KC)], sqy_b, stdc_t, stdc_b, rstdc_t, rstdc_b)
        for k in range(KC):
            i = k % 2
            P.dve(lambda: V.scalar_tensor_tensor(
                out=t12_t[i][:, :], in0=y_t[:, k, :], scalar=G[:, 3, k:k + 1], in1=rstdc_t[:, :],
                op0=ALU.mult, op1=ALU.mult), [y_b[k], rstdc_b, G_b], [t12_b[i]])
            P.dve(lambda: V.tensor_tensor(out=xt[:, k, :], in0=xt[:, k, :], in1=t12_t[i][:, :], op=ALU.add),
                  [xbb[k], t12_b[i]], [xbb[k]])
        P.dma(P.SP, xs23[:, :, tb * TB:(tb + 1) * TB], xt[:, :, :], reads=xbb, writes=[xs2_b[tb]])
    P.barrier()

    out_b = P.buf("out")

    def tailD(b, c, part):
        if part == 0:
            xt, xbb = c["xb_t"][b % 2], c["xb_b"][b % 2]
            P.dma(P.SP, outT3[:, :, b * TB:(b + 1) * TB], xt[:, :, :], reads=xbb, writes=[out_b])

    AR.reset(m_const)
    new_slots(5, 5632)
    ffn_phase(OWN // TB, lambda b: xs23[:, :, b * TB:(b + 1) * TB], 2, 4, 1, tailD)
    P.SP.wait((out_b.dsem, out_b.dcnt, "d_out"))
    P.barrier()
    return nc


_CACHE = {}


def prep_inputs(inp):
    x = np.asarray(inp["x"], np.float32)
    _, pk = weight_tiles({k: np.asarray(v, np.float32) for k, v in inp.items()})
    gl = ["ffn1_pre_g", "ffn1_post_g", "mix_pre_g", "mix_post_g", "ffn2_pre_g", "ffn2_post_g"]
    gains = np.stack([np.asarray(inp[n], np.float32)[0].reshape(KC, 128).T for n in gl], axis=1)
    gains = np.ascontiguousarray(gains).reshape(128, -1)
    rpb = np.asarray(inp["na_rpb"], np.float32)[0]
    in_maps = []
    for c in range(NCORES):
        b, qd = c // 4, c % 4
        lo = OWN * qd - HALO
        xe = np.zeros((EXT, D), np.float32)
        s0, s1 = max(lo, 0), min(lo + EXT, T)
        xe[s0 - lo:s1 - lo] = x[b, s0:s1]
        xTc = np.ascontiguousarray(xe.reshape(EXT, KC, 128).transpose(2, 1, 0)).reshape(128, -1)
        in_maps.append({"xT": xTc, "wpk": pk, "gains": gains,
                        "nab": na_bias_table(rpb, qd), "dlb": dil_bias_table(qd)})
    return in_maps


def kernel(**inputs):
    if "nc" not in _CACHE:
        _CACHE["nc"] = build_program(False)
    nc = _CACHE["nc"]
    in_maps = prep_inputs(inputs)
    res = run_bass_kernel_spmd(nc, in_maps, core_ids=list(range(NCORES)))
    out = np.empty((2, T, D), np.float32)
    for c in range(NCORES):
        b, qd = c // 4, c % 4
        o = np.asarray(res.results[c]["outT"]).reshape(128, KC, OWN)
        out[b, OWN * qd:OWN * (qd + 1)] = o.transpose(2, 1, 0).reshape(OWN, D)
    return out
```
